# Optimizing a Trainium2 kernel written in Bass

```python
import jax, jax.numpy as jnp
from jax import lax
import numpy as np

D_MODEL = 1024
BATCH = 8
SEQ = 2048
DEPTH = 4

GRID_W = 64
CTX_LEN = 256
EPS = 1e-6
NEG_BIG = -1e30
LB_FLOOR = 1e-30

N_BRANCH = 3
BRANCH_WIDTH = D_MODEL // 2

FOURIER_GROUPS = 4
FOURIER_WIDTH = BRANCH_WIDTH
FOURIER_GROUP_DIM = FOURIER_WIDTH // FOURIER_GROUPS

HEAD_DIM = 64
ATTN_HEADS = BRANCH_WIDTH // HEAD_DIM
ATTN_KV_HEADS = ATTN_HEADS // 4
ATTN_WIDTH = ATTN_HEADS * HEAD_DIM
KV_WIDTH = ATTN_KV_HEADS * HEAD_DIM
WINDOW = 128
BLOCK = 128
ROPE_THETA = 10000.0

HGRN_HEADS = 4
HGRN_DK = BRANCH_WIDTH // HGRN_HEADS
HGRN_DV = BRANCH_WIDTH // HGRN_HEADS
HGRN_KW = HGRN_HEADS * HGRN_DK
HGRN_VW = HGRN_HEADS * HGRN_DV
CHUNK = 64

D_FF = 4 * D_MODEL

_SEG_SIZES = (KV_WIDTH, KV_WIDTH, HGRN_VW, HGRN_KW, HGRN_KW,
              ATTN_WIDTH, HGRN_KW, HGRN_VW, FOURIER_WIDTH, N_BRANCH * D_MODEL)
IN_SPLITS = [sum(_SEG_SIZES[:i + 1]) for i in range(len(_SEG_SIZES) - 1)]
D_IN = sum(_SEG_SIZES)
N_STATE_COLS = sum(_SEG_SIZES[:5])
STATE_SPLITS = IN_SPLITS[:4]

kernel_name = "hybrid_fourier_swa_hgrn2_dit_prefix"

F32 = jnp.float32


def rms_norm(x, g):
    xf = x.astype(F32)
    y = xf * lax.rsqrt(jnp.mean(xf * xf, axis=-1, keepdims=True) + EPS)
    return (y * g.astype(F32)).astype(x.dtype)


def adaln_norm(x, g, shift, scale):
    return rms_norm(x, g) * (1 + scale) + shift


def modulation(cond, w, b):
    m = jax.nn.silu(cond) @ w + b
    return jnp.split(m, 6, axis=-1)


def _heads(a, n):
    return a.reshape(a.shape[:-1] + (n, a.shape[-1] // n))


def _to_bhtd(a, n):
    return jnp.swapaxes(_heads(a, n), 1, 2)


def _flip_t(a):
    return jnp.flip(a, axis=2)


def axial_rope_tables(n_tok):
    n_rows = n_tok // GRID_W
    row = jnp.repeat(jnp.arange(n_rows), GRID_W).astype(F32)
    col = jnp.tile(jnp.arange(GRID_W), n_rows).astype(F32)
    axis_dim = HEAD_DIM // 2
    inv_freq = ROPE_THETA ** (-jnp.arange(0, axis_dim, 2, dtype=F32) / axis_dim)
    ang_r = row[:, None] * inv_freq
    ang_c = col[:, None] * inv_freq
    return (jnp.cos(ang_r), jnp.sin(ang_r), jnp.cos(ang_c), jnp.sin(ang_c))


def apply_axial_rope(t, rope):
    cos_r, sin_r, cos_c, sin_c = rope

    def rot(u, cos, sin):
        u1, u2 = jnp.split(u, 2, axis=-1)
        cos = cos[None, :, None, :]
        sin = sin[None, :, None, :]
        return jnp.concatenate([u1 * cos - u2 * sin, u2 * cos + u1 * sin], axis=-1)

    tr, tc = jnp.split(t, 2, axis=-1)
    return jnp.concatenate([rot(tr, cos_r, sin_r), rot(tc, cos_c, sin_c)], axis=-1).astype(t.dtype)


def _sink_column(sink, ref):
    g = ATTN_HEADS // ATTN_KV_HEADS
    return jnp.broadcast_to(sink.astype(F32).reshape(ATTN_KV_HEADS, g, 1, 1), ref.shape[:-1] + (1,))


def window_attention(q, k, v, kc, vc, sink):
    b, s, _, dh = q.shape
    nb = s // BLOCK
    g = ATTN_HEADS // ATTN_KV_HEADS
    scale = dh ** -0.5
    qb = q.reshape(b, nb, BLOCK, ATTN_KV_HEADS, g, dh)
    pad = ((0, 0), (BLOCK, BLOCK), (0, 0), (0, 0))
    kp = jnp.pad(k, pad).reshape(b, nb + 2, BLOCK, ATTN_KV_HEADS, dh)
    vp = jnp.pad(v, pad).reshape(b, nb + 2, BLOCK, ATTN_KV_HEADS, dh)
    kw = jnp.concatenate([kp[:, :-2], kp[:, 1:-1], kp[:, 2:]], axis=2)
    vw = jnp.concatenate([vp[:, :-2], vp[:, 1:-1], vp[:, 2:]], axis=2)
    s_win = jnp.einsum('bnqhgd,bnkhd->bnhgqk', qb, kw).astype(F32) * scale
    s_ctx = jnp.einsum('bnqhgd,bchd->bnhgqc', qb, kc).astype(F32) * scale
    t_pos = jnp.arange(nb)[:, None, None] * BLOCK + jnp.arange(BLOCK)[None, :, None]
    k_pos = jnp.arange(nb)[:, None, None] * BLOCK - BLOCK + jnp.arange(3 * BLOCK)[None, None, :]
    valid = (jnp.abs(k_pos - t_pos) <= WINDOW) & (k_pos >= 0) & (k_pos < s)
    s_win = jnp.where(valid[None, :, None, None], s_win, NEG_BIG)
    logits = jnp.concatenate([s_win, s_ctx, _sink_column(sink, s_win)], axis=-1)
    p = jax.nn.softmax(logits, axis=-1)
    nw = 3 * BLOCK
    nc = kc.shape[1]
    o = (jnp.einsum('bnhgqk,bnkhd->bnqhgd', p[..., :nw].astype(v.dtype), vw)
         + jnp.einsum('bnhgqc,bchd->bnqhgd', p[..., nw:nw + nc].astype(v.dtype), vc))
    return o.reshape(b, s, ATTN_HEADS * dh)


def context_attention(qc, kc, vc, sink):
    b, lc, _, dh = qc.shape
    g = ATTN_HEADS // ATTN_KV_HEADS
    qg = qc.reshape(b, lc, ATTN_KV_HEADS, g, dh)
    s = jnp.einsum('bqhgd,bkhd->bhgqk', qg, kc).astype(F32) * dh ** -0.5
    p = jax.nn.softmax(jnp.concatenate([s, _sink_column(sink, s)], axis=-1), axis=-1)
    o = jnp.einsum('bhgqk,bkhd->bqhgd', p[..., :lc].astype(vc.dtype), vc)
    return o.reshape(b, lc, ATTN_HEADS * dh)


def fourier_mix(a):
    b, t, _ = a.shape
    ag = a.astype(F32).reshape(b, t, FOURIER_GROUPS, FOURIER_GROUP_DIM)
    y = jnp.fft.fft2(ag, axes=(1, 3), norm='ortho').real
    return y.reshape(b, t, FOURIER_WIDTH).astype(a.dtype)


def hgrn_forget(z, lb):
    zf = z.astype(F32)
    lbf = lb.astype(F32)
    k = (1.0 - lbf) * jax.nn.sigmoid(-zf)
    logf = jnp.logaddexp(jax.nn.log_sigmoid(zf),
                         jnp.log(jnp.maximum(lbf, LB_FLOOR)) + jax.nn.log_sigmoid(-zf))
    return k, logf


def hgrn_kv(i_raw, ff_raw, fb_raw, lb):
    v = _to_bhtd(i_raw.astype(F32), HGRN_HEADS)
    kf, lff = hgrn_forget(ff_raw, lb[0])
    kb, lfb = hgrn_forget(fb_raw, lb[1])
    return (v, _to_bhtd(kf, HGRN_HEADS), _to_bhtd(lff, HGRN_HEADS),
            _to_bhtd(kb, HGRN_HEADS), _to_bhtd(lfb, HGRN_HEADS))


def hgrn_q(q_raw):
    return _to_bhtd(jax.nn.silu(q_raw).astype(F32), HGRN_HEADS)


def gla_chunked(q, k, v, logf, s0):
    b, h, t, dk = q.shape
    dv = v.shape[-1]
    n = t // CHUNK

    def to_chunks(a):
        return jnp.moveaxis(a.reshape(b, h, n, CHUNK, a.shape[-1]), 2, 0)

    lower = jnp.tril(jnp.ones((CHUNK, CHUNK), dtype=bool))[:, :, None]

    def step(state, inp):
        qc, kc, vc, lc = inp
        g = jnp.cumsum(lc, axis=2)
        o_inter = jnp.einsum('bhtk,bhkv->bhtv', qc * jnp.exp(g), state)
        diff = g[:, :, :, None, :] - g[:, :, None, :, :]
        decay = jnp.where(lower, jnp.exp(jnp.where(lower, diff, 0.0)), 0.0)
        a = jnp.einsum('bhtsk,bhsk->bhts', qc[:, :, :, None, :] * decay, kc)
        o = o_inter + jnp.einsum('bhts,bhsv->bhtv', a, vc)
        g_last = g[:, :, -1:, :]
        new_state = (jnp.exp(g_last[:, :, 0, :])[..., None] * state
                     + jnp.einsum('bhsk,bhsv->bhkv', kc * jnp.exp(g_last - g), vc))
        return new_state, o

    s_fin, o = lax.scan(step, s0, (to_chunks(q), to_chunks(k), to_chunks(v), to_chunks(logf)))
    o = jnp.moveaxis(o, 0, 2).reshape(b, h, t, dv)
    return o, s_fin


def gla_final_state(k, v, logf):
    g = jnp.cumsum(logf, axis=2)
    return jnp.einsum('bhsk,bhsv->bhkv', k * jnp.exp(g[:, :, -1:, :] - g), v)


def hgrn_bidir(q, kf, lff, kb, lfb, v, s_f, s_b):
    o_f, s_f_new = gla_chunked(q, kf, v, lff, s_f)
    o_b, s_b_new = gla_chunked(_flip_t(q), _flip_t(kb), _flip_t(v), _flip_t(lfb), s_b)
    return o_f + _flip_t(o_b), s_f_new, s_b_new


def hgrn_readout(o, g_raw, norm_g):
    o = rms_norm(jnp.swapaxes(o, 1, 2), norm_g)
    return o.reshape(g_raw.shape).astype(g_raw.dtype) * jax.nn.silu(g_raw)


def merge_branches(o_four, o_attn, o_hgrn, gate_raw, w_branch, w_out):
    br = jnp.stack([o_four, o_attn, o_hgrn], axis=-2)
    proj = jnp.einsum('btnc,ncd->btnd', br, w_branch)
    gates = jax.nn.sigmoid(gate_raw.reshape(gate_raw.shape[:-1] + (N_BRANCH, D_MODEL)))
    return jnp.sum(gates * proj, axis=-2) @ w_out


def sq_relu_mlp(h, w1, w2):
    return jnp.square(jax.nn.relu(h @ w1)) @ w2


def setup_inputs(seed: int = 0) -> dict:
    key = jax.random.key(seed)
    ks = jax.random.split(key, 18)

    def nrm(k, shape, s):
        return jax.random.normal(k, shape, F32) * s

    return {
        "x": nrm(ks[0], (BATCH, SEQ, D_MODEL), 1.0),
        "c": nrm(ks[1], (BATCH, D_MODEL), 1.0),
        "ctx": nrm(ks[2], (BATCH, CTX_LEN, D_MODEL), 1.0),
        "c_ctx": nrm(ks[3], (D_MODEL,), 1.0),
        "w_mod": nrm(ks[4], (DEPTH, D_MODEL, 6 * D_MODEL), 0.5 * D_MODEL ** -0.5),
        "b_mod": nrm(ks[5], (DEPTH, 6 * D_MODEL), 0.02),
        "norm1_g": 1.0 + nrm(ks[6], (DEPTH, D_MODEL), 0.02),
        "norm2_g": 1.0 + nrm(ks[7], (DEPTH, D_MODEL), 0.02),
        "w_in": nrm(ks[8], (DEPTH, D_MODEL, D_IN), D_MODEL ** -0.5),
        "q_norm_g": 1.0 + nrm(ks[9], (DEPTH, HEAD_DIM), 0.02),
        "k_norm_g": 1.0 + nrm(ks[10], (DEPTH, HEAD_DIM), 0.02),
        "attn_sink": nrm(ks[11], (DEPTH, ATTN_HEADS), 0.5),
        "hgrn_lb_logits": nrm(ks[12], (DEPTH, 2, HGRN_KW), 0.5),
        "hgrn_norm_g": 1.0 + nrm(ks[13], (DEPTH, HGRN_DV), 0.02),
        "w_branch": nrm(ks[14], (DEPTH, N_BRANCH, BRANCH_WIDTH, D_MODEL), BRANCH_WIDTH ** -0.5),
        "w_out": nrm(ks[15], (DEPTH, D_MODEL, D_MODEL), D_MODEL ** -0.5),
        "w_ff1": nrm(ks[16], (DEPTH, D_MODEL, D_FF), D_MODEL ** -0.5),
        "w_ff2": nrm(ks[17], (DEPTH, D_FF, D_MODEL), D_FF ** -0.5),
    }


def reference(x, c, ctx, c_ctx, w_mod, b_mod, norm1_g, norm2_g, w_in, q_norm_g, k_norm_g,
              attn_sink, hgrn_lb_logits, hgrn_norm_g, w_branch, w_out, w_ff1, w_ff2):
    bsz, n_tok, _ = x.shape
    rope = axial_rope_tables(n_tok)
    lb_p = jax.nn.softmax(hgrn_lb_logits.astype(F32), axis=0)
    lb_all = jnp.cumsum(lb_p, axis=0) - lb_p[0:1]
    xc = ctx
    for l in range(DEPTH):
        last = l == DEPTH - 1
        mx = [m[:, None, :] for m in modulation(c, w_mod[l], b_mod[l])]
        mc = modulation(c_ctx, w_mod[l], b_mod[l])
        h = adaln_norm(x, norm1_g[l], mx[0], mx[1])
        hc = adaln_norm(xc, norm1_g[l], mc[0], mc[1])

        if last:
            kc_raw, vc_raw, ic_raw, fcf_raw, fcb_raw = jnp.split(
                hc @ w_in[l][:, :N_STATE_COLS], STATE_SPLITS, axis=-1)
        else:
            (kc_raw, vc_raw, ic_raw, fcf_raw, fcb_raw, qc_raw, qhc_raw, ghc_raw,
             fourc_raw, gatec_raw) = jnp.split(hc @ w_in[l], IN_SPLITS, axis=-1)
        kc = rms_norm(_heads(kc_raw, ATTN_KV_HEADS), k_norm_g[l])
        vc = _heads(vc_raw, ATTN_KV_HEADS)
        vhc, kfc, lffc, kbc, lfbc = hgrn_kv(ic_raw, fcf_raw, fcb_raw, lb_all[l])
        if last:
            s_f = gla_final_state(kfc, vhc, lffc)
            s_b = gla_final_state(_flip_t(kbc), _flip_t(vhc), _flip_t(lfbc))
        else:
            zeros = jnp.zeros((bsz, HGRN_HEADS, HGRN_DK, HGRN_DV), F32)
            oc_h, s_f, s_b = hgrn_bidir(hgrn_q(qhc_raw), kfc, lffc, kbc, lfbc, vhc, zeros, zeros)
            qc = rms_norm(_heads(qc_raw, ATTN_HEADS), q_norm_g[l])
            oc_a = context_attention(qc, kc, vc, attn_sink[l])
            oc_f = fourier_mix(fourc_raw)
            oc_hr = hgrn_readout(oc_h, ghc_raw, hgrn_norm_g[l])
            yc = merge_branches(oc_f, oc_a, oc_hr, gatec_raw, w_branch[l], w_out[l])
            xc = xc + mc[2] * yc
            hc2 = adaln_norm(xc, norm2_g[l], mc[3], mc[4])
            xc = xc + mc[5] * sq_relu_mlp(hc2, w_ff1[l], w_ff2[l])

        (k_raw, v_raw, i_raw, ff_raw, fb_raw, q_raw, qh_raw, gh_raw,
         four_raw, gate_raw) = jnp.split(h @ w_in[l], IN_SPLITS, axis=-1)
        q = apply_axial_rope(rms_norm(_heads(q_raw, ATTN_HEADS), q_norm_g[l]), rope)
        k = apply_axial_rope(rms_norm(_heads(k_raw, ATTN_KV_HEADS), k_norm_g[l]), rope)
        v = _heads(v_raw, ATTN_KV_HEADS)
        o_a = window_attention(q, k, v, kc, vc, attn_sink[l])
        vh, kf, lff, kb, lfb = hgrn_kv(i_raw, ff_raw, fb_raw, lb_all[l])
        o_h, _, _ = hgrn_bidir(hgrn_q(qh_raw), kf, lff, kb, lfb, vh, s_f, s_b)
        o_hr = hgrn_readout(o_h, gh_raw, hgrn_norm_g[l])
        o_f = fourier_mix(four_raw)
        y = merge_branches(o_f, o_a, o_hr, gate_raw, w_branch[l], w_out[l])
        x = x + mx[2] * y
        h2 = adaln_norm(x, norm2_g[l], mx[3], mx[4])
        x = x + mx[5] * sq_relu_mlp(h2, w_ff1[l], w_ff2[l])
    return x
```

```python
import os
import numpy as np
import ml_dtypes
import concourse.bass as bass
import concourse.mybir as mybir
from concourse.bass_utils import run_bass_kernel_spmd
from contextlib import ExitStack

F32 = mybir.dt.float32
BF16 = mybir.dt.bfloat16
AF = mybir.ActivationFunctionType
ALU = mybir.AluOpType

D = 1024
NL = 4
S_LAT = 2048
S_CTX = 256
NT = S_LAT + S_CTX
NTILE = NT // 128
TBS = [(0, 512), (512, 512), (1024, 512), (1536, 512), (2048, 256)]
EPS = 1e-6
DIN = 6912
SEG_K, SEG_V, SEG_I, SEG_FF, SEG_FB, SEG_Q, SEG_QH, SEG_GH, SEG_FOUR, SEG_GATE = (
    0, 128, 256, 768, 1280, 1792, 2304, 2816, 3328, 3840)

DEBUG = {}


class Sched:
    CE = ("pe", "act", "dve", "pool")

    def __init__(self, nc, es):
        self.nc = nc
        self.es = es
        self.q = {e: [] for e in ("pe", "act", "dve", "pool", "sp")}
        self.semh = {}
        self.cnt = {}
        for e in self.CE:
            self.semh[e] = es.enter_context(nc.semaphore("s_" + e))
            self.cnt[e] = 0
        self.known = {e: {} for e in self.q}
        self.lastw = {}
        self.readers = {}
        self.nops = 0

    def dsem(self, name):
        if name not in self.semh:
            self.semh[name] = self.es.enter_context(self.nc.semaphore("d_" + name))
            self.cnt[name] = 0
        return name

    def _need(self, eng, ev, waits):
        if ev is None:
            return
        sk, val = ev
        if self.known[eng].get(sk, 0) >= val:
            return
        if waits.get(sk, 0) < val:
            waits[sk] = val

    def op(self, eng, fns, reads=(), writes=(), dsem=None):
        if not isinstance(fns, (list, tuple)):
            fns = [fns]
        waits = {}
        for k in reads:
            self._need(eng, self.lastw.get(k), waits)
        for k in writes:
            self._need(eng, self.lastw.get(k), waits)
            for ev in self.readers.get(k, ()):
                self._need(eng, ev, waits)
        for sk, v in waits.items():
            self.q[eng].append(("w", sk, v))
            self.known[eng][sk] = v
        if dsem is None:
            sk = eng
            self.cnt[eng] += 1
            inc = 1
        else:
            sk = self.dsem(dsem)
            self.cnt[sk] += 16 * len(fns)
            inc = 16
        val = self.cnt[sk]
        self.q[eng].append(("i", fns, sk, inc))
        ev = (sk, val)
        for k in reads:
            self.readers.setdefault(k, []).append(ev)
        for k in writes:
            self.lastw[k] = ev
            self.readers[k] = []
        self.nops += len(fns)

    def barrier(self):
        for e in self.CE + ("sp",):
            for f in self.CE:
                if f != e and self.known[e].get(f, 0) < self.cnt[f]:
                    self.q[e].append(("w", f, self.cnt[f]))
                    self.known[e][f] = self.cnt[f]

    def wait_all(self, eng, sems):
        for sk in sems:
            if self.known[eng].get(sk, 0) < self.cnt[sk]:
                self.q[eng].append(("w", sk, self.cnt[sk]))
                self.known[eng][sk] = self.cnt[sk]

    def replay(self, eng, e):
        for it in self.q[eng]:
            if it[0] == "w":
                e.wait_ge(self.semh[it[1]], it[2])
            else:
                _, fns, sk, inc = it
                n = len(fns)
                for i, f in enumerate(fns):
                    ins = f(e)
                    if inc == 16 or i == n - 1:
                        ins.then_inc(self.semh[sk], inc)


class Arena:
    def __init__(self, base_bf16, nbytes):
        self.base = base_bf16
        self.n = nbytes
        self.top = 0

    def alloc(self, nbytes):
        nbytes = (nbytes + 63) // 64 * 64
        off = self.top
        self.top += nbytes
        self.peak = max(getattr(self, "peak", 0), self.top)
        assert self.top <= self.n, f"arena overflow {self.top} > {self.n}"
        return off

    def view(self, off, dt, *free):
        n = int(np.prod(free))
        es = 4 if dt == F32 else 2
        ap = self.base[:, off // 2: off // 2 + n * es // 2]
        if dt == F32:
            ap = ap.bitcast(F32)
        if len(free) > 1:
            names = "abcde"[: len(free)]
            pat = "p (" + " ".join(names) + ") -> p " + " ".join(names)
            ap = ap.rearrange(pat, **{names[i]: free[i] for i in range(len(free))})
        return ap

    def new(self, dt, *free):
        es = 4 if dt == F32 else 2
        off = self.alloc(int(np.prod(free)) * es)
        return self.view(off, dt, *free)


def _bf(a):
    return np.asarray(a, dtype=np.float32).astype(ml_dtypes.bfloat16)


C16_IDENT, C16_ONES, C16_BONES, C16_PERM, C16_MPREV, C16_MNEXT = 0, 128, 256, 384, 512, 640
C16_CCSC, C16_GLAF, C16_GLAB, C16_DFTC = 768, 1024, 1088, 1152
C16_SCAN = 1152 + 1024
C16_N = 1152 + 1024 + 512
R16_COS, R16_SIN, R16_SCAN = 0, NT, 2 * NT
R16_N = 3 * NT


def make_consts():
    c = np.zeros((128, C16_N), np.float64)
    c[:, C16_IDENT:C16_IDENT + 128] = np.eye(128)
    c[:, C16_ONES:C16_ONES + 128] = 1.0
    bo = np.zeros((128, 128))
    bo[:64, :64] = 1
    bo[64:, 64:] = 1
    c[:, C16_BONES:C16_BONES + 128] = bo
    perm = np.zeros((128, 128))
    for dout in range(128):
        d = dout % 64
        partner = d + 16 if (d % 32) < 16 else d - 16
        perm[(dout // 64) * 64 + partner, dout] = 1
    c[:, C16_PERM:C16_PERM + 128] = perm
    j = np.arange(128)[:, None]
    i = np.arange(128)[None, :]
    c[:, C16_MPREV:C16_MPREV + 128] = (j >= i)
    c[:, C16_MNEXT:C16_MNEXT + 128] = (j <= i)
    cc = np.arange(128)[:, None] * np.arange(128)[None, :] % 128
    c[:, C16_CCSC:C16_CCSC + 128] = np.cos(2 * np.pi * cc / 128) / np.sqrt(128)
    c[:, C16_CCSC + 128:C16_CCSC + 256] = np.sin(2 * np.pi * cc / 128) / np.sqrt(128)
    s = np.arange(64)[:, None]
    t = np.arange(64)[None, :]
    blk = (s // 32) == (t // 32)
    c[:, C16_GLAF:C16_GLAF + 64] = np.tile((s <= t) & blk, (2, 1))
    c[:, C16_GLAB:C16_GLAB + 64] = np.tile((s >= t) & blk, (2, 1))
    smk = np.ones(512)
    smk[::64] = 0.0
    c[:, C16_SCAN:C16_SCAN + 512] = smk[None, :]
    tt = np.arange(256)[:, None] * np.arange(256)[None, :] % 256
    ct = np.cos(2 * np.pi * tt / 256) / np.sqrt(256)
    st = -np.sin(2 * np.pi * tt / 256) / np.sqrt(256)
    c[:, C16_DFTC:C16_DFTC + 512] = ct.reshape(2, 128, 256).transpose(1, 0, 2).reshape(128, 512)
    c[:, C16_DFTC + 512:C16_DFTC + 1024] = st.reshape(2, 128, 256).transpose(1, 0, 2).reshape(128, 512)
    r = np.zeros((128, R16_N), np.float64)
    inv_freq = 10000.0 ** (-(np.arange(0, 32, 2, dtype=np.float32) / np.float32(32))).astype(np.float64)
    tok = np.arange(S_LAT)
    row = (tok // 64).astype(np.float64)
    col = (tok % 64).astype(np.float64)
    for p in range(128):
        d = p % 64
        f = inv_freq[d % 16]
        pos = row if d < 32 else col
        ang = (pos.astype(np.float32) * np.float32(f)).astype(np.float64)
        sign = -1.0 if (d % 32) < 16 else 1.0
        r[p, R16_COS:R16_COS + S_CTX] = 1.0
        r[p, R16_COS + S_CTX:R16_COS + NT] = np.cos(ang)
        r[p, R16_SIN + S_CTX:R16_SIN + NT] = sign * np.sin(ang)
    sm = np.ones(NT)
    sm[::64] = 0.0
    r[:, R16_SCAN:R16_SCAN + NT] = sm[None, :]
    k = np.arange(S_LAT, dtype=np.int64)
    m = (k[:, None] * k[None, :]) % S_LAT
    dc = np.cos(2 * np.pi * m / S_LAT) / np.sqrt(S_LAT)
    ds = -np.sin(2 * np.pi * m / S_LAT) / np.sqrt(S_LAT)
    return _bf(c), _bf(r), _bf(dc), _bf(ds), np.eye(128, dtype=np.float32)


V_N1G, V_N2G, V_BMOD, V_QG, V_KG, V_SINK, V_LB, V_HGN, V_CC = 0, 32, 64, 256, 260, 264, 296, 328, 332
V_N = 348


def make_vecs(c_b, c_ctx, b_mod, norm1_g, norm2_g, q_norm_g, k_norm_g, attn_sink, hgrn_lb_logits, hgrn_norm_g):
    v = np.zeros((128, V_N), np.float32)
    v[:, V_N1G:V_N1G + 32] = norm1_g.reshape(NL, 8, 128).transpose(2, 0, 1).reshape(128, 32)
    v[:, V_N2G:V_N2G + 32] = norm2_g.reshape(NL, 8, 128).transpose(2, 0, 1).reshape(128, 32)
    v[:, V_BMOD:V_BMOD + 192] = b_mod.reshape(NL, 48, 128).transpose(2, 0, 1).reshape(128, 192)
    v[:, V_QG:V_QG + 4] = np.tile(q_norm_g.T, (2, 1))
    v[:, V_KG:V_KG + 4] = np.tile(k_norm_g.T, (2, 1))
    sp = attn_sink.reshape(NL, 8)
    v[:, V_SINK:V_SINK + 32] = np.broadcast_to(sp.reshape(1, 32), (128, 32))
    v[:, V_LB:V_LB + 32] = hgrn_lb_logits.reshape(NL, 2, 4, 128).transpose(3, 0, 1, 2).reshape(128, 32)
    v[:, V_HGN:V_HGN + 4] = hgrn_norm_g.T
    cc = np.stack([c_ctx.reshape(8, 128).T, c_b.reshape(8, 128).T], axis=-1)
    v[:, V_CC:V_CC + 16] = cc.reshape(128, 16)
    return v


def build(nlayers=NL, dbg=False, stop_after=None, only=None):
    nc = bass.Bass("TRN2", target_bir_lowering=False)
    es = ExitStack()
    dt_in = {}

    def din(name, shape, dt=F32):
        t = nc.dram_tensor(name, list(shape), dt, kind="ExternalInput").ap()
        dt_in[name] = t
        return t

    x_d = din("x", [S_LAT, D])
    ctx_d = din("ctx", [S_CTX, D])
    vecs_d = din("vecs", [128, V_N])
    identf_d = din("identf", [128, 128])
    c16_d = din("c16", [128, C16_N], BF16)
    r16_d = din("r16", [128, R16_N], BF16)
    dftc_d = din("dftc", [S_LAT, S_LAT], BF16)
    dfts_d = din("dfts", [S_LAT, S_LAT], BF16)
    wmod_d = din("w_mod", [NL, D, 6 * D])
    win_d = din("w_in", [NL, D, DIN])
    wbr_d = din("w_branch", [NL, 3, 512, D])
    wout_d = din("w_out", [NL, D, D])
    wff1_d = din("w_ff1", [NL, D, 4 * D])
    wff2_d = din("w_ff2", [NL, 4 * D, D])
    out_d = nc.dram_tensor("out", [S_LAT, D], F32, kind="ExternalOutput").ap()
    dbg_out = {}

    ARENA_BYTES = 212000 // 64 * 64
    arena_t = es.enter_context(nc.sbuf_tensor("arena", [128, ARENA_BYTES // 2], BF16))
    A = Arena(arena_t, ARENA_BYTES)
    psb = [es.enter_context(nc.psum_tensor(f"ps{i}", [128, 512], F32)) for i in range(8)]
    S = Sched(nc, es)

    def PS(i):
        return psb[i][:, :]

    def pk(i):
        return f"ps{i}"

    xT = A.new(F32, 8, NT)
    hT = A.new(BF16, 8, NT)
    modv = A.new(F32, NL, 48, 2)
    gs1 = A.new(F32, NL, 8, 2)
    gs2 = A.new(F32, NL, 8, 2)
    vecs = A.new(F32, V_N)
    c16 = A.new(BF16, C16_N)
    cst = A.new(F32, 8)
    lbv = A.new(F32, NL, 8)
    lbm = A.new(F32, NL, 8)
    nlb = A.new(F32, NL, 8)
    lbf = A.new(F32, NL, 8)
    lbg = A.new(F32, NL, 8)
    lbe = A.new(F32, NL, 8)
    lbs = A.new(F32, 8)
    esink = A.new(F32, NL, 8)
    qg8 = A.new(F32, NL)
    PERSIST_TOP = A.top

    ident = c16[:, C16_IDENT:C16_IDENT + 128]
    ones = c16[:, C16_ONES:C16_ONES + 128]
    bones = c16[:, C16_BONES:C16_BONES + 128]
    perm = c16[:, C16_PERM:C16_PERM + 128]

    def dma(out, in_):
        return lambda e: e.dma_start(out=out, in_=in_)

    def mm(out, lhsT, rhs, start, stop):
        return lambda e: e.matmul(out, lhsT=lhsT, rhs=rhs, start=start, stop=stop)

    def act(out, in_, func, bias=None, scale=None):
        kw = {}
        if bias is not None:
            kw["bias"] = bias
        if scale is not None:
            kw["scale"] = scale
        return lambda e: e.activation(out=out, in_=in_, func=func, **kw)

    def tt(out, in0, in1, op):
        return lambda e: e.tensor_tensor(out=out, in0=in0, in1=in1, op=op)

    def ts(out, in0, s1, s2, op0, op1=None):
        if op1 is None:
            return lambda e: e.tensor_scalar(out=out, in0=in0, scalar1=s1, scalar2=None, op0=op0)
        return lambda e: e.tensor_scalar(out=out, in0=in0, scalar1=s1, scalar2=s2, op0=op0, op1=op1)

    def stt(out, in0, scalar, in1, op0, op1):
        return lambda e: e.scalar_tensor_tensor(out=out, in0=in0, scalar=scalar, in1=in1, op0=op0, op1=op1)

    def cp(out, in_):
        return lambda e: e.tensor_copy(out=out, in_=in_)

    ndbg = [0]

    def dump(name, ap, shape, reads):
        if not dbg:
            return
        t = nc.dram_tensor("dbg_" + name, [128] + list(shape), ap.dtype, kind="ExternalOutput").ap()
        dbg_out[name] = t
        S.barrier()
        S.op("sp", [dma(t, ap)], reads=reads, dsem="dbg")
        for e_ in S.CE:
            S.wait_all(e_, ["dbg"])

    def tbk(name, tb):
        return (name, tb)

    def tile_tb(t):
        return min(t // 4, 4)

    S.op("sp", [dma(vecs, vecs_d[:, :])], writes=["vecs"], dsem="c_vecs")
    S.op("sp", [dma(c16, c16_d[:, :])], writes=["c16"], dsem="c_c16")
    S.op("dve", [lambda e: e.memset(cst[:, 0:1], EPS)], writes=["cst"])


    lbl = vecs[:, V_LB:V_LB + 32].rearrange("p (l c) -> p l c", l=NL)
    S.op("act", [act(lbe, lbl, AF.Exp)], reads=["vecs"], writes=["lbe"])
    S.op("dve", [tt(lbs, lbe[:, 0, :], lbe[:, 1, :], ALU.add)], reads=["lbe"], writes=["lbs"])
    S.op("dve", [tt(lbs, lbs, lbe[:, 2, :], ALU.add)], reads=["lbe", "lbs"], writes=["lbs"])
    S.op("dve", [tt(lbs, lbs, lbe[:, 3, :], ALU.add)], reads=["lbe", "lbs"], writes=["lbs"])
    S.op("dve", [lambda e: e.reciprocal(out=lbs, in_=lbs)], reads=["lbs"], writes=["lbs"])
    S.op("dve", [tt(lbe, lbe, lbs.unsqueeze(1).to_broadcast([128, NL, 8]), ALU.mult)], reads=["lbe", "lbs"], writes=["lbe"])
    S.op("dve", [lambda e: e.memset(lbv[:, 0, :], 0.0)], writes=["lb0"])
    S.op("dve", [cp(lbv[:, 1, :], lbe[:, 1, :])], reads=["lbe"], writes=["lb1"])
    S.op("dve", [tt(lbv[:, 2, :], lbv[:, 1, :], lbe[:, 2, :], ALU.add)], reads=["lbe", "lb1"], writes=["lb2"])
    S.op("dve", [tt(lbv[:, 3, :], lbv[:, 2, :], lbe[:, 3, :], ALU.add)], reads=["lbe", "lb2"], writes=["lb3"])
    S.op("dve", [ts(lbm, lbv, -1.0, 1.0, ALU.mult, ALU.add)], reads=["lb0", "lb1", "lb2", "lb3"], writes=["lbm"])
    S.op("dve", [ts(nlb, lbv, -1.0, None, ALU.add)], reads=["lb0", "lb1", "lb2", "lb3"], writes=["nlb"])
    S.op("dve", [ts(lbf, lbv, 1e-30, None, ALU.max)], reads=["lb0", "lb1", "lb2", "lb3"], writes=["lbf"])
    S.op("dve", [ts(lbg, lbf, -1.0, 1.0, ALU.mult, ALU.add)], reads=["lbf"], writes=["lbg"])

    mark = A.top
    identf = A.new(F32, 128)
    xs = [A.new(F32, D), A.new(F32, D)]
    S.op("sp", [dma(identf, identf_d[:, :])], writes=["identf"], dsem="c_identf")
    for t in range(NTILE):
        src = ctx_d[t * 128:(t + 1) * 128, :] if t < 2 else x_d[(t - 2) * 128:(t - 1) * 128, :]
        b = t % 2
        S.op("sp", [dma(xs[b], src)], writes=[f"xs{b}"], dsem=f"xs{b}")
        for half in range(2):
            bi = (2 * t + half) % 4
            fns = [mm(PS(bi)[:, j * 128:(j + 1) * 128], xs[b][:, (half * 4 + j) * 128:(half * 4 + j + 1) * 128],
                      identf, True, True) for j in range(4)]
            S.op("pe", fns, reads=[f"xs{b}", "identf"], writes=[pk(bi)])
            dst = xT[:, half * 4:half * 4 + 4, t * 128:(t + 1) * 128]
            srcp = PS(bi).rearrange("p (a b) -> p a b", a=4)
            eng = "act" if half == 0 else "dve"
            if eng == "act":
                S.op("act", [act(dst, srcp, AF.Copy)], reads=[pk(bi)], writes=[("xTt", t, half)])
            else:
                S.op("dve", [cp(dst, srcp)], reads=[pk(bi)], writes=[("xTt", t, half)])
    S.barrier()
    A.top = mark

    mark = A.top
    wm = [A.new(F32, 8, 512), A.new(F32, 8, 512)]
    sc = A.new(F32, 8, 2)
    ccv = vecs[:, V_CC:V_CC + 16].rearrange("p (a b) -> p a b", a=8)
    S.op("act", [act(sc, ccv, AF.Silu)], reads=["vecs"], writes=["sc"])
    bmod = vecs[:, V_BMOD:V_BMOD + 192].rearrange("p (l c) -> p l c", l=NL)
    for l in range(NL):
        pb = 4 + (l % 2)
        psm = PS(pb)[:, 0:96].rearrange("p (c t) -> p c t", t=2)
        for sb in range(12):
            i = l * 12 + sb
            slot = wm[i % 2]
            S.op("sp", [dma(slot, wmod_d[l, :, sb * 512:(sb + 1) * 512].rearrange("(k p) c -> p k c", p=128))],
                 writes=[f"wm{i % 2}"], dsem=f"wm{i % 2}")
            for c4 in range(4):
                cch = sb * 4 + c4
                fns = [mm(psm[:, cch, :], slot[:, kc, c4 * 128:(c4 + 1) * 128], sc[:, kc, :], kc == 0, kc == 7)
                       for kc in range(8)]
                S.op("pe", fns, reads=[f"wm{i % 2}", "sc"], writes=[pk(pb)])
        S.op("dve", [tt(modv[:, l, :, :], psm, bmod[:, l, :].unsqueeze(2).to_broadcast([128, 48, 2]), ALU.add)],
             reads=[pk(pb), "vecs"], writes=[("modv", l)])
        n1g = vecs[:, V_N1G + l * 8:V_N1G + l * 8 + 8].unsqueeze(2).to_broadcast([128, 8, 2])
        n2g = vecs[:, V_N2G + l * 8:V_N2G + l * 8 + 8].unsqueeze(2).to_broadcast([128, 8, 2])
        S.op("dve", [stt(gs1[:, l, :, :], modv[:, l, 8:16, :], 1.0, n1g, ALU.add, ALU.mult)],
             reads=[("modv", l)], writes=[("gs1", l)])
        S.op("dve", [stt(gs2[:, l, :, :], modv[:, l, 32:40, :], 1.0, n2g, ALU.add, ALU.mult)],
             reads=[("modv", l)], writes=[("gs2", l)])
    S.barrier()
    A.top = mark

    def mvec(l, m, kc, cond):
        return modv[:, l, m * 8 + kc, cond:cond + 1]

    def norm_stage(l, which):
        gs = gs1 if which == 1 else gs2
        msh = 0 if which == 1 else 3
        mark = A.top
        rstd = [A.new(F32, 512), A.new(F32, 512)]
        tmp = [A.new(F32, 512), A.new(F32, 512)]
        for tb, (c0, n) in enumerate(TBS):
            cols = slice(c0, c0 + n)
            S.op("act", [act(hT[:, :, cols], xT[:, :, cols], AF.Square)],
                 reads=[("x", tb)], writes=[("h", tb)])
            pb = tb % 2
            S.op("pe", [mm(PS(pb)[:, 0:n], ones, hT[:, kc, cols], kc == 0, kc == 7) for kc in range(8)],
                 reads=[("h", tb), "c16"], writes=[pk(pb)])
            r = rstd[tb % 2]
            S.op("act", [act(r[:, 0:n], PS(pb)[:, 0:n], AF.Sqrt, bias=cst[:, 0:1], scale=1.0 / D)],
                 reads=[pk(pb), "cst"], writes=[f"rstd{tb % 2}"])
            S.op("dve", [lambda e, r=r, n=n: e.reciprocal(out=r[:, 0:n], in_=r[:, 0:n])],
                 reads=[f"rstd{tb % 2}"], writes=[f"rstd{tb % 2}"])
            for kc in range(8):
                tm = tmp[kc % 2]
                S.op("dve", [tt(tm[:, 0:n], xT[:, kc, cols], r[:, 0:n], ALU.mult)],
                     reads=[("x", tb), f"rstd{tb % 2}"], writes=[f"ntmp{kc % 2}"])
                fns = []
                if tb == 0:
                    fns.append(act(hT[:, kc, 0:256], tm[:, 0:256], AF.Identity,
                                   bias=mvec(l, msh, kc, 0), scale=gs[:, l, kc, 0:1]))
                    fns.append(act(hT[:, kc, 256:512], tm[:, 256:512], AF.Identity,
                                   bias=mvec(l, msh, kc, 1), scale=gs[:, l, kc, 1:2]))
                else:
                    fns.append(act(hT[:, kc, cols], tm[:, 0:n], AF.Identity,
                                   bias=mvec(l, msh, kc, 1), scale=gs[:, l, kc, 1:2]))
                S.op("act", fns, reads=[f"ntmp{kc % 2}", (f"gs{which}", l), ("modv", l)], writes=[("h", tb)])
        S.barrier()
        A.top = mark


    def xupdate(ps_ap, jo, tb, c0, n, l, mg, rkey):
        fns = []
        if tb == 0:
            fns.append(stt(xT[:, jo, 0:256], ps_ap[:, 0:256], mvec(l, mg, jo, 0), xT[:, jo, 0:256], ALU.mult, ALU.add))
            fns.append(stt(xT[:, jo, 256:512], ps_ap[:, 256:512], mvec(l, mg, jo, 1), xT[:, jo, 256:512], ALU.mult, ALU.add))
        else:
            fns.append(stt(xT[:, jo, c0:c0 + n], ps_ap[:, 0:n], mvec(l, mg, jo, 1), xT[:, jo, c0:c0 + n], ALU.mult, ALU.add))
        S.op("dve", fns, reads=[rkey], writes=[("x", tb, jo)])

    def wview(src2d, k):
        return src2d.rearrange("(k p) c -> p k c", p=128)

    def merge_stage(l, n_br, oT):
        mark = A.top
        wb = A.new(BF16, 4, 1024)
        wg = A.new(BF16, 8, 1024)
        wo = A.new(BF16, 8, 1024)
        mbuf = [A.new(BF16, 8, 512), A.new(BF16, 8, 512)]
        sig = [A.new(BF16, 512), A.new(BF16, 512)]
        S.op("pool", [dma(wb, wview(wbr_d[l, n_br, :, :], 4))], writes=["wb"], dsem="wb")
        for hf in range(2):
            S.op("pool", [dma(wg[:, hf * 4:(hf + 1) * 4, :],
                              wview(win_d[l, hf * 512:(hf + 1) * 512, SEG_GATE + n_br * 1024:SEG_GATE + (n_br + 1) * 1024], 4))],
                 writes=[f"wg{hf}"], dsem=f"wg{hf}")
        for hf in range(2):
            S.op("pool", [dma(wo[:, hf * 4:(hf + 1) * 4, :], wview(wout_d[l, hf * 512:(hf + 1) * 512, :], 4))],
                 writes=[f"wo{hf}"], dsem=f"wo{hf}")
        for tb, (c0, n) in enumerate(TBS):
            cols = slice(c0, c0 + n)
            mb = mbuf[tb % 2]
            for j in range(8):
                b1 = j % 2
                b2 = 2 + j % 2
                js = slice(j * 128, (j + 1) * 128)
                S.op("pe", [mm(PS(b1)[:, 0:n], wb[:, kc, js], oT[:, kc, cols], kc == 0, kc == 3) for kc in range(4)],
                     reads=["wb", "oT"], writes=[pk(b1)])
                S.op("pe", [mm(PS(b2)[:, 0:n], wg[:, kc, js], hT[:, kc, cols], kc == 0, kc == 7) for kc in range(8)],
                     reads=["wg0", "wg1", ("h", tb)], writes=[pk(b2)])
                sg = sig[j % 2]
                S.op("act", [act(sg[:, 0:n], PS(b2)[:, 0:n], AF.Sigmoid)], reads=[pk(b2)], writes=[f"sig{j % 2}"])
                S.op("dve", [tt(mb[:, j, 0:n], PS(b1)[:, 0:n], sg[:, 0:n], ALU.mult)],
                     reads=[pk(b1), f"sig{j % 2}"], writes=[(f"mb{tb % 2}", j)])
            for jo in range(8):
                b3 = 4 + jo % 2
                S.op("pe", [mm(PS(b3)[:, 0:n], wo[:, j, jo * 128:(jo + 1) * 128], mb[:, j, 0:n], j == 0, j == 7)
                            for j in range(8)],
                     reads=["wo0", "wo1"] + [(f"mb{tb % 2}", j) for j in range(8)], writes=[pk(b3)])
                xupdate(PS(b3), jo, tb, c0, n, l, 2, pk(b3))
        S.barrier()
        A.top = mark

    def ffn_stage(l):
        mark = A.top
        ub = [A.new(BF16, 4, NT), A.new(BF16, 4, NT)]
        w1 = [A.new(BF16, 8, 512), A.new(BF16, 8, 512)]
        w2 = [A.new(BF16, 4, 1024), A.new(BF16, 4, 1024)]
        rt = [A.new(BF16, 512), A.new(BF16, 512)]
        cnt = 0
        for g in range(8):
            b = g % 2
            S.op("pool", [dma(w1[b], wview(wff1_d[l, :, g * 512:(g + 1) * 512], 8))], writes=[f"w1{b}"], dsem=f"w1{b}")
            S.op("pool", [dma(w2[b], wview(wff2_d[l, g * 512:(g + 1) * 512, :], 4))], writes=[f"w2{b}"], dsem=f"w2{b}")
            for hc in range(4):
                for tb, (c0, n) in enumerate(TBS):
                    cols = slice(c0, c0 + n)
                    pb = cnt % 4
                    r = rt[cnt % 2]
                    S.op("pe", [mm(PS(pb)[:, 0:n], w1[b][:, kc, hc * 128:(hc + 1) * 128], hT[:, kc, cols], kc == 0, kc == 7)
                                for kc in range(8)], reads=[f"w1{b}", ("h", tb)], writes=[pk(pb)])
                    S.op("act", [act(r[:, 0:n], PS(pb)[:, 0:n], AF.Relu)], reads=[pk(pb)], writes=[f"rt{cnt % 2}"])
                    S.op("dve", [tt(ub[b][:, hc, cols], r[:, 0:n], r[:, 0:n], ALU.mult)],
                         reads=[f"rt{cnt % 2}"], writes=[(f"ub{b}", hc, tb)])
                    cnt += 1
            for j in range(8):
                for tb, (c0, n) in enumerate(TBS):
                    cols = slice(c0, c0 + n)
                    pb = 4 + cnt % 4
                    S.op("pe", [mm(PS(pb)[:, 0:n], w2[b][:, kc, j * 128:(j + 1) * 128], ub[b][:, kc, cols], kc == 0, kc == 3)
                                for kc in range(4)],
                         reads=[f"w2{b}"] + [(f"ub{b}", kc, tb) for kc in range(4)], writes=[pk(pb)])
                    xupdate(PS(pb), j, tb, c0, n, l, 5, pk(pb))
                    cnt += 1
        S.barrier()
        A.top = mark

    def fourier_stage(l, oT):
        mark = A.top
        wf = A.new(BF16, 8, 512)
        Ab = [A.new(BF16, NT), A.new(BF16, NT)]
        Z = A.new(BF16, NTILE, 256)
        dslot = [A.new(BF16, 2, 16, 256), A.new(BF16, 2, 16, 256)]
        ccsc = c16[:, C16_CCSC:C16_CCSC + 256]
        dctx = c16[:, C16_DFTC:C16_DFTC + 1024].rearrange("p (a k c) -> p a k c", a=2, k=2)
        S.op("pool", [dma(wf, wview(win_d[l, :, SEG_FOUR:SEG_FOUR + 512], 8))], writes=["wf"], dsem="wf")
        nslot = 0
        for g in range(4):
            Ag = Ab[g % 2]
            for tb, (c0, n) in enumerate(TBS):
                cols = slice(c0, c0 + n)
                pb = tb % 2
                S.op("pe", [mm(PS(pb)[:, 0:n], wf[:, kc, g * 128:(g + 1) * 128], hT[:, kc, cols], kc == 0, kc == 7)
                            for kc in range(8)], reads=["wf", ("h", tb)], writes=[pk(pb)])
                S.op("act", [act(Ag[:, cols], PS(pb)[:, 0:n], AF.Copy)], reads=[pk(pb)], writes=[(f"A{g % 2}", tb)])
            for tp in range(NTILE // 2):
                pb = 2 + tp % 2
                fns = [mm(PS(pb)[:, j * 256:(j + 1) * 256], Ag[:, (2 * tp + j) * 128:(2 * tp + j + 1) * 128], ccsc, True, True)
                       for j in range(2)]
                S.op("pe", fns, reads=[(f"A{g % 2}", tile_tb(2 * tp)), (f"A{g % 2}", tile_tb(2 * tp + 1)), "c16"], writes=[pk(pb)])
                S.op("dve", [cp(Z[:, 2 * tp:2 * tp + 2, :], PS(pb).rearrange("p (a b) -> p a b", a=2))],
                     reads=[pk(pb)], writes=[("Z", tp)])
            pb = 4
            fns = []
            for k in range(2):
                fns.append(mm(PS(pb)[:, 0:256], Z[:, k, 0:128], dctx[:, 0, k, :], k == 0, False))
                fns.append(mm(PS(pb)[:, 0:256], Z[:, k, 128:256], dctx[:, 1, k, :], False, k == 1))
            S.op("pe", fns, reads=[("Z", 0), "c16"], writes=[pk(pb)])
            S.op("act", [act(oT[:, g, 0:256], PS(pb)[:, 0:256], AF.Copy)], reads=[pk(pb)], writes=["oT"])
            for cb in range(8):
                sl = nslot % 2
                nslot += 1
                ds = dslot[sl]
                S.op("sp", [dma(ds[:, 0, :, :], dftc_d[:, cb * 256:(cb + 1) * 256].rearrange("(k p) c -> p k c", p=128)),
                            dma(ds[:, 1, :, :], dfts_d[:, cb * 256:(cb + 1) * 256].rearrange("(k p) c -> p k c", p=128))],
                     writes=[f"ds{sl}"], dsem=f"ds{sl}")
                pb = 5 + cb % 2
                fns = []
                for k in range(16):
                    fns.append(mm(PS(pb)[:, 0:256], Z[:, 2 + k, 0:128], ds[:, 0, k, :], k == 0, False))
                    fns.append(mm(PS(pb)[:, 0:256], Z[:, 2 + k, 128:256], ds[:, 1, k, :], False, k == 15))
                S.op("pe", fns, reads=[f"ds{sl}"] + [("Z", tp) for tp in range(1, 9)], writes=[pk(pb)])
                S.op("act", [act(oT[:, g, 256 + cb * 256:256 + (cb + 1) * 256], PS(pb)[:, 0:256], AF.Copy)],
                     reads=[pk(pb)], writes=["oT"])
        S.barrier()
        A.top = mark

    def attention_stage(l, oT):
        mark = A.top
        rope = A.new(BF16, 2, NT)
        wq = A.new(BF16, 8, 512)
        wk2 = A.new(BF16, 8, 2, 2, 64)
        wv = A.new(BF16, 8, 128)
        qT4 = A.new(BF16, 4, NT)
        tmpq = [A.new(BF16, 512), A.new(BF16, 512)]
        kT = A.new(BF16, NT)
        vtok = A.new(BF16, NTILE, 65)
        sqb = [A.new(BF16, 512), A.new(BF16, 512)]
        rsb = [A.new(F32, 512), A.new(F32, 512)]
        qnb = [A.new(BF16, 512), A.new(BF16, 512)]
        t1b = [A.new(F32, 512)]
        t2b = [A.new(F32, 512)]
        pTb = [A.new(BF16, 512) for _ in range(6)]
        den = [A.new(F32, 4), A.new(F32, 4)]
        otok = [A.new(BF16, 4, 64), A.new(BF16, 4, 64)]
        S.op("sp", [dma(rope, r16_d[:, 0:2 * NT].rearrange("p (a t) -> p a t", a=2))], writes=["rope"], dsem="rope")
        S.op("pool", [dma(wq, wview(win_d[l, :, SEG_Q:SEG_Q + 512], 8))], writes=["wq"], dsem="wq")
        S.op("pool", [dma(wk2[:, :, g_, du, :], wview(win_d[l, :, SEG_K + g_ * 64:SEG_K + (g_ + 1) * 64], 8))
                      for g_ in range(2) for du in range(2)], writes=["wk2"], dsem="wk2")
        S.op("pool", [dma(wv, wview(win_d[l, :, SEG_V:SEG_V + 128], 8))], writes=["wv"], dsem="wv")
        S.op("act", [act(esink[:, l, :], vecs[:, V_SINK + l * 8:V_SINK + l * 8 + 8], AF.Exp)], reads=["vecs"], writes=["esink"])
        S.op("dve", [ts(qg8[:, l:l + 1], vecs[:, V_QG + l:V_QG + l + 1], 0.125, None, ALU.mult)], reads=["vecs"], writes=["qg8"])
        mprev = c16[:, C16_MPREV:C16_MPREV + 128]
        mnext = c16[:, C16_MNEXT:C16_MNEXT + 128]
        cntp = [0]

        def prep(lhs_of_kc, dst_e, dst_o, gvec, wkeys, dkey):
            for tb, (c0, n) in enumerate(TBS):
                cols = slice(c0, c0 + n)
                i = cntp[0]
                cntp[0] += 1
                p1, p2, p3 = i % 2, 2 + i % 2, 4 + i % 2
                S.op("pe", [mm(PS(p1)[:, 0:n], lhs_of_kc(kc), hT[:, kc, cols], kc == 0, kc == 7) for kc in range(8)],
                     reads=wkeys + [("h", tb)], writes=[pk(p1)])
                S.op("act", [act(sqb[i % 2][:, 0:n], PS(p1)[:, 0:n], AF.Square)], reads=[pk(p1)], writes=[f"sqb{i % 2}"])
                S.op("pe", [mm(PS(p2)[:, 0:n], bones, sqb[i % 2][:, 0:n], True, True)], reads=[f"sqb{i % 2}", "c16"], writes=[pk(p2)])
                r = rsb[i % 2]
                S.op("act", [act(r[:, 0:n], PS(p2)[:, 0:n], AF.Sqrt, bias=cst[:, 0:1], scale=1.0 / 64)],
                     reads=[pk(p2), "cst"], writes=[f"rsb{i % 2}"])
                S.op("dve", [lambda e, r=r, n=n: e.reciprocal(out=r[:, 0:n], in_=r[:, 0:n])], reads=[f"rsb{i % 2}"], writes=[f"rsb{i % 2}"])
                qn = qnb[i % 2]
                S.op("dve", [stt(qn[:, 0:n], PS(p1)[:, 0:n], gvec, r[:, 0:n], ALU.mult, ALU.mult)],
                     reads=[pk(p1), f"rsb{i % 2}", "qg8", "vecs"], writes=[f"qnb{i % 2}"])
                S.op("pe", [mm(PS(p3)[:, 0:n], perm, qn[:, 0:n], True, True)], reads=[f"qnb{i % 2}", "c16"], writes=[pk(p3)])
                S.op("dve", [tt(t1b[0][:, 0:n], qn[:, 0:n], rope[:, 0, cols], ALU.mult)],
                     reads=[f"qnb{i % 2}", "rope"], writes=["t1b"])
                S.op("dve", [tt(t2b[0][:, 0:n], PS(p3)[:, 0:n], rope[:, 1, cols], ALU.mult)],
                     reads=[pk(p3), "rope"], writes=["t2b"])
                tq_ = tmpq[i % 2]
                S.op("dve", [tt(tq_[:, 0:n], t1b[0][:, 0:n], t2b[0][:, 0:n], ALU.add)],
                     reads=["t1b", "t2b"], writes=[f"tmpq{i % 2}"])
                S.op("act", [act(dst_e[0:64, cols], tq_[0:64, 0:n], AF.Copy)], reads=[f"tmpq{i % 2}"], writes=[(dkey, 0, tb)])
                if dst_o is not None:
                    S.op("pe", [mm(PS(p2)[0:64, 0:n], ident[:, 64:128], tq_[:, 0:n], True, True)],
                         reads=[f"tmpq{i % 2}", "c16"], writes=[pk(p2)])
                    S.op("act", [act(dst_o[0:64, cols], PS(p2)[0:64, 0:n], AF.Copy)], reads=[pk(p2)], writes=[(dkey, 1, tb)])

        cnts = 0
        ALV = int(os.environ.get('ATT_LEVEL', '9'))
        for gi in range(2):
            if ALV < 1:
                break
            for pi in range(2):
                pr = 2 * gi + pi
                prep(lambda kc, pr=pr: wq[:, kc, pr * 128:(pr + 1) * 128], qT4[:, 2 * pi, :], qT4[:, 2 * pi + 1, :],
                     qg8[:, l:l + 1], ["wq"], f"qT{pi}")
            prep(lambda kc, gi=gi: wk2[:, kc, gi, :, :].rearrange("p a d -> p (a d)"), kT, None, vecs[:, V_KG + l:V_KG + l + 1], ["wk2"], "kT")
            if ALV < 2:
                continue
            S.op("dve", [lambda e: e.memset(vtok[:, :, 64:65], 1.0)], writes=["vones"])
            for t0 in range(0, NTILE, 8):
                nt_ = min(8, NTILE - t0)
                pb = 6 + (t0 // 8) % 2
                fns = []
                for j in range(nt_):
                    t = t0 + j
                    for kc in range(8):
                        fns.append(mm(PS(pb)[:, j * 64:(j + 1) * 64], hT[:, kc, t * 128:(t + 1) * 128],
                                      wv[:, kc, gi * 64:(gi + 1) * 64], kc == 0, kc == 7))
                S.op("pe", fns, reads=["wv"] + [("h", tb) for tb in range(5)], writes=[pk(pb)])
                S.op("act", [act(vtok[:, t0:t0 + nt_, 0:64], PS(pb)[:, 0:nt_ * 64].rearrange("p (a b) -> p a b", b=64), AF.Copy)],
                     reads=[pk(pb)], writes=[("vtok", t0 // 8)])
            if ALV < 3:
                continue
            vkeys = [("vtok", i) for i in range(3)] + ["vones"]
            qkeys = [(f"qT{pi}", e_, tb) for pi in range(2) for e_ in range(2) for tb in range(5)]
            kkeys = [("kT", 0, tb) for tb in range(5)]
            for t in range(NTILE):
                if t < 2:
                    kts = [(0, None), (1, None)]
                else:
                    kts = [(0, None), (1, None)]
                    if t - 1 >= 2:
                        kts.append((t - 1, mprev))
                    kts.append((t, None))
                    if t + 1 <= NTILE - 1:
                        kts.append((t + 1, mnext))
                po = 6 + t % 2
                pts = []
                for idx, (kt, msk) in enumerate(kts):
                    sb_ = cnts % 2
                    pT = pTb[cnts % len(pTb)]
                    pkey = f"pT{cnts % len(pTb)}"
                    cnts += 1
                    S.op("pe", [mm(PS(sb_)[:, 0:512], kT[0:64, kt * 128:(kt + 1) * 128], qT4[0:64, :, t * 128:(t + 1) * 128], True, True)],
                         reads=qkeys + kkeys, writes=[pk(sb_)])
                    S.op("act", [act(pT, PS(sb_), AF.Exp)], reads=[pk(sb_)], writes=[pkey])
                    if msk is not None and os.environ.get('ATT_NOMASK') is None:
                        pv = pT.rearrange("p (a b) -> p a b", a=4)
                        S.op("dve", [tt(pv, pv, msk.unsqueeze(1).to_broadcast([128, 4, 128]), ALU.mult)],
                             reads=[pkey, "c16"], writes=[pkey])
                    pts.append((pT, pkey, kt))
                if ALV < 4:
                    continue
                fns = []
                for s_ in range(4):
                    for idx, (pT, pkey, kt) in enumerate(pts):
                        fns.append(mm(PS(po)[:, s_ * 65:(s_ + 1) * 65], pT[:, s_ * 128:(s_ + 1) * 128], vtok[:, kt, :],
                                      idx == 0, idx == len(pts) - 1))
                S.op("pe", fns, reads=[p[1] for p in pts] + vkeys, writes=[pk(po)])
                if ALV < 5:
                    continue
                dn = den[t % 2]
                pov = PS(po)[:, 0:260].rearrange("p (a c) -> p a c", a=4)
                S.op("dve", [tt(dn, pov[:, :, 64], esink[:, l, gi * 4:(gi + 1) * 4], ALU.add)],
                     reads=[pk(po), "esink"], writes=[f"den{t % 2}"])
                S.op("dve", [lambda e, dn=dn: e.reciprocal(out=dn, in_=dn)], reads=[f"den{t % 2}"], writes=[f"den{t % 2}"])
                ot = otok[t % 2]
                S.op("dve", [tt(ot, pov[:, :, 0:64], dn.unsqueeze(2).to_broadcast([128, 4, 64]), ALU.mult)],
                     reads=[pk(po), f"den{t % 2}"], writes=[f"otok{t % 2}"])
                ptb = 4 + t % 2
                fns = [mm(PS(ptb)[:, pi * 128:(pi + 1) * 128], ot[:, 2 * pi:2 * pi + 2, :].rearrange("p e d -> p (e d)"), ident, True, True)
                       for pi in range(2)]
                S.op("pe", fns, reads=[f"otok{t % 2}", "c16"], writes=[pk(ptb)])
                S.op("act", [act(oT[:, 2 * gi:2 * gi + 2, t * 128:(t + 1) * 128],
                                 PS(ptb)[:, 0:256].rearrange("p (a b) -> p a b", a=2), AF.Copy)],
                     reads=[pk(ptb)], writes=["oT"])
        S.barrier()
        A.top = mark


    def hgrn_stage(l, oT):
        mark = A.top
        wh = A.new(BF16, 8, 5, 128)
        V64 = A.new(BF16, 36, 128)
        oacc = A.new(F32, NT)
        sg = [A.new(F32, 512), A.new(F32, 512)]
        kraw = A.new(F32, 512)
        qs = A.new(F32, 512)
        lf = A.new(F32, 512)
        gb = A.new(F32, 512)
        Eb = [A.new(F32, 512), A.new(F32, 512)]
        Ub = A.new(F32, 512)
        ex = [A.new(F32, 512), A.new(F32, 512)]
        opb = [[A.new(BF16, 512) for _ in range(6)] for _ in range(2)]
        gL = A.new(F32, 8)
        glx = [A.new(F32, 8), A.new(F32, 8)]
        bref = A.new(F32, 8)
        mref = A.new(F32, 16)
        ktok = [A.new(BF16, 128) for _ in range(3)]
        At = [A.new(BF16, 64) for _ in range(3)]
        Sst = A.new(F32, 128)
        Sb = [A.new(BF16, 128), A.new(BF16, 128)]
        sqo = [A.new(BF16, 512), A.new(BF16, 512)]
        rr = [A.new(F32, 512), A.new(F32, 512)]
        smask = c16[:, C16_SCAN:C16_SCAN + 512]
        glaf = c16[:, C16_GLAF:C16_GLAF + 64]
        glab = c16[:, C16_GLAB:C16_GLAB + 64]
        hkeys = [("h", tb) for tb in range(5)]
        cnt = 0
        gcount = 0
        S.op("dve", [lambda e: e.memset(V64[64:128, :, :], 0.0)] +
             [lambda e, a_=a_: e.memset(a_[64:128, :], 0.0) for a_ in At] +
             [lambda e, a_=a_: e.memset(a_[64:128, :], 0.0) for a_ in ktok], writes=["zpad"])
        for hh in range(4):
            S.op("pool", [dma(wh[:, :, si, :], wview(win_d[l, :, seg + hh * 128:seg + (hh + 1) * 128], 8))
                          for si, seg in enumerate((SEG_I, SEG_FF, SEG_FB, SEG_QH, SEG_GH))], writes=["wh"], dsem="wh")
            for c4 in range(0, 36, 4):
                pb = (c4 // 4) % 2
                fns = [mm(PS(pb)[0:64, j * 128:(j + 1) * 128], hT[:, kc, (c4 + j) * 64:(c4 + j + 1) * 64], wh[:, kc, 0, :],
                          kc == 0, kc == 7) for j in range(4) for kc in range(8)]
                S.op("pe", fns, reads=["wh"] + hkeys, writes=[pk(pb)])
                S.op("act", [act(V64[0:64, c4:c4 + 4, :], PS(pb)[0:64, :].rearrange("p (a b) -> p a b", b=128), AF.Copy)],
                     reads=[pk(pb)], writes=["V"])
            HL = int(os.environ.get('HG_LEVEL', '9'))
            for di in range(2):
                if HL < 2:
                    break
                fwd = di == 0
                sgn = 1.0 if fwd else -1.0
                li = di * 4 + hh
                lbm_ = lbm[:, l, li:li + 1]
                nlb_ = nlb[:, l, li:li + 1]
                lbf_ = lbf[:, l, li:li + 1]
                lbg_ = lbg[:, l, li:li + 1]
                S.op("dve", [lambda e: e.memset(Sst, 0.0)], writes=["Sst"])
                S.op("act", [act(Sb[0], Sst, AF.Copy)], reads=["Sst"], writes=["Sb0"])
                sbi = 0
                if fwd:
                    groups = [(tb, list(range(tb * 8, min(tb * 8 + 8, 36)))) for tb in range(5)]
                else:
                    groups = [(0, [3, 2, 1, 0]), (4, [35, 34, 33, 32])] + \
                             [(tb, list(range(tb * 8 + 7, tb * 8 - 1, -1))) for tb in (3, 2, 1)] + [(0, [7, 6, 5, 4])]
                mask = glaf if fwd else glab
                mi = (15, 47) if fwd else (16, 48)
                bi_ = 31 if fwd else 32
                for tb, chunks in groups:
                    c0, n = TBS[tb]
                    cols = slice(c0, c0 + n)
                    nch = n // 64
                    gs_ = gcount % 2
                    gcount += 1
                    ops_ = opb[gs_]
                    okeys = [f"op{gs_}_{k}" for k in range(6)]
                    i = cnt
                    cnt += 1
                    pb = 2 + i % 2
                    S.op("pe", [mm(PS(pb)[:, 0:n], wh[:, kc, 1 + di, :], hT[:, kc, cols], kc == 0, kc == 7) for kc in range(8)],
                         reads=["wh", ("h", tb)], writes=[pk(pb)])
                    sg_ = sg[i % 2]
                    S.op("act", [act(sg_[:, 0:n], PS(pb)[:, 0:n], AF.Sigmoid)], reads=[pk(pb)], writes=[f"sg{i % 2}"])
                    S.op("dve", [ts(kraw[:, 0:n], sg_[:, 0:n], nlb_, lbm_, ALU.mult, ALU.add)],
                         reads=[f"sg{i % 2}", "lbm", "nlb"], writes=["kraw"])
                    S.op("act", [act(lf[:, 0:n], sg_[:, 0:n], AF.Ln, bias=lbf_, scale=lbg_)],
                         reads=[f"sg{i % 2}", "lbf", "lbg"], writes=["lf"])
                    S.op("dve", [lambda e, n=n: e.tensor_tensor_scan(out=gb[:, 0:n], data0=smask[:, 0:n], data1=lf[:, 0:n],
                                                                     initial=0.0, op0=ALU.mult, op1=ALU.add)],
                         reads=["lf", "c16"], writes=["gb"])
                    gv = gb[:, 0:n].rearrange("p (c t) -> p c t", t=64)
                    S.op("dve", [cp(gL[:, 0:nch], gv[:, :, 63])], reads=["gb"], writes=["gL"])
                    glx_ = glx[gs_]
                    S.op("act", [act(glx_[:, 0:nch], gL[:, 0:nch], AF.Exp)], reads=["gL"], writes=[f"glx{gs_}"])
                    if not fwd:
                        S.op("dve", [tt(gb[:, 0:n], gb[:, 0:n], lf[:, 0:n], ALU.subtract)], reads=["gb", "lf"], writes=["gb"])
                    S.op("dve", [cp(bref[:, 0:nch], gv[:, :, bi_])], reads=["gb"], writes=["bref"])
                    mv = mref[:, 0:2 * nch].rearrange("p (c a) -> p c a", a=2)
                    S.op("dve", [cp(mv[:, :, 0], gv[:, :, mi[0]]), cp(mv[:, :, 1], gv[:, :, mi[1]])], reads=["gb"], writes=["mref"])
                    i = cnt
                    cnt += 1
                    pb = 2 + i % 2
                    S.op("pe", [mm(PS(pb)[:, 0:n], wh[:, kc, 3, :], hT[:, kc, cols], kc == 0, kc == 7) for kc in range(8)],
                         reads=["wh", ("h", tb)], writes=[pk(pb)])
                    S.op("act", [act(qs[:, 0:n], PS(pb)[:, 0:n], AF.Silu)], reads=[pk(pb)], writes=["qs"])
                    gLb = gL[:, 0:nch].unsqueeze(2).to_broadcast([128, nch, 64])
                    E0 = Eb[0]
                    S.op("dve", [tt(E0[:, 0:n].rearrange("p (c t) -> p c t", t=64), gLb, gv, ALU.subtract)],
                         reads=["gL", "gb"], writes=["Eb0"])
                    src_q = gb if fwd else E0
                    src_k = E0 if fwd else gb
                    kq = "gb" if fwd else "Eb0"
                    kk_ = "Eb0" if fwd else "gb"
                    S.op("act", [act(ex[0][:, 0:n], src_q[:, 0:n], AF.Exp)], reads=[kq], writes=["ex0"])
                    S.op("dve", [tt(ops_[0][:, 0:n], qs[:, 0:n], ex[0][:, 0:n], ALU.mult)], reads=["qs", "ex0"], writes=[okeys[0]])
                    S.op("act", [act(ex[1][:, 0:n], src_k[:, 0:n], AF.Exp)], reads=[kk_], writes=["ex1"])
                    S.op("dve", [tt(ops_[1][:, 0:n], kraw[:, 0:n], ex[1][:, 0:n], ALU.mult)], reads=["kraw", "ex1"], writes=[okeys[1]])
                    E1 = Eb[1]
                    S.op("dve", [tt(E1[:, 0:n].rearrange("p (c t) -> p c t", t=32), gb[:, 0:n].rearrange("p (c t) -> p c t", t=32),
                                    mref[:, 0:2 * nch].unsqueeze(2).to_broadcast([128, 2 * nch, 32]), ALU.subtract)],
                         reads=["gb", "mref"], writes=["Eb1"])
                    S.op("act", [act(ex[0][:, 0:n], E1[:, 0:n], AF.Exp, scale=sgn)], reads=["Eb1"], writes=["ex0"])
                    S.op("dve", [tt(ops_[2][:, 0:n], qs[:, 0:n], ex[0][:, 0:n], ALU.mult)], reads=["qs", "ex0"], writes=[okeys[2]])
                    S.op("act", [act(ex[1][:, 0:n], E1[:, 0:n], AF.Exp, scale=-sgn)], reads=["Eb1"], writes=["ex1"])
                    S.op("dve", [tt(ops_[3][:, 0:n], kraw[:, 0:n], ex[1][:, 0:n], ALU.mult)], reads=["kraw", "ex1"], writes=[okeys[3]])
                    S.op("dve", [tt(Ub[:, 0:n].rearrange("p (c t) -> p c t", t=64), gv,
                                    bref[:, 0:nch].unsqueeze(2).to_broadcast([128, nch, 64]), ALU.subtract)],
                         reads=["gb", "bref"], writes=["Ub"])
                    S.op("dve", [ts(E0[:, 0:n], Ub[:, 0:n], sgn, 0.0, ALU.mult, ALU.min)], reads=["Ub"], writes=["Eb0"])
                    S.op("act", [act(ex[0][:, 0:n], E0[:, 0:n], AF.Exp)], reads=["Eb0"], writes=["ex0"])
                    hq = slice(32, 64) if fwd else slice(0, 32)
                    hk = slice(0, 32) if fwd else slice(32, 64)
                    v64 = lambda ap_: ap_[:, 0:n].rearrange("p (c t) -> p c t", t=64)
                    S.op("dve", [lambda e, a_=v64(ops_[4])[:, :, hk]: e.memset(a_, 0.0),
                                 tt(v64(ops_[4])[:, :, hq], v64(qs)[:, :, hq], v64(ex[0])[:, :, hq], ALU.mult)],
                         reads=["qs", "ex0"], writes=[okeys[4]])
                    S.op("dve", [ts(E1[:, 0:n], Ub[:, 0:n], -sgn, 0.0, ALU.mult, ALU.min)], reads=["Ub"], writes=["Eb1"])
                    S.op("act", [act(ex[1][:, 0:n], E1[:, 0:n], AF.Exp)], reads=["Eb1"], writes=["ex1"])
                    S.op("dve", [lambda e, a_=v64(ops_[5])[:, :, hq]: e.memset(a_, 0.0),
                                 tt(v64(ops_[5])[:, :, hk], v64(kraw)[:, :, hk], v64(ex[1])[:, :, hk], ALU.mult)],
                         reads=["kraw", "ex1"], writes=[okeys[5]])
                    qh_, kh_, qd_, kd_, qo_, ko_ = ops_
                    po = 6 + gs_
                    for ci, c in enumerate(chunks):
                        if HL < 3:
                            break
                        j = c - tb * 8
                        cj = slice(j * 64, (j + 1) * 64)
                        ai = cnt % 3
                        cnt += 1
                        pa = 2 + ai % 2
                        S.op("pe", [mm(PS(pa)[0:64, 0:64], kd_[:, cj], qd_[:, cj], True, True),
                                    mm(PS(pa)[0:64, 64:128], ko_[:, cj], qo_[:, cj], True, True)],
                             reads=[okeys[2], okeys[3], okeys[4], okeys[5]], writes=[pk(pa)])
                        pt_ = ai % 2
                        S.op("pe", [mm(PS(pt_)[0:64, 0:128], kh_[:, cj], ident, True, True)],
                             reads=[okeys[1], "c16"], writes=[pk(pt_)])
                        at = At[ai]
                        HSUB = os.environ.get('HG_SUB', 'z')
                        if HSUB == 'a':
                            continue
                        S.op("dve", [tt(at[0:64, 0:64], PS(pa)[0:64, 0:64], mask[0:64, :], ALU.mult)],
                             reads=[pk(pa), "c16"], writes=[f"At{ai}"])
                        S.op("dve", [tt(at[0:64, 0:64], at[0:64, 0:64], PS(pa)[0:64, 64:128], ALU.add)],
                             reads=[pk(pa), f"At{ai}"], writes=[f"At{ai}"])
                        kt_ = ktok[ai]
                        if HSUB == 'b':
                            continue
                        S.op("act", [act(kt_[0:64, :], PS(pt_)[0:64, 0:128], AF.Copy)], reads=[pk(pt_)], writes=[f"ktok{ai}"])
                        if HL < 4:
                            continue
                        pcs = slice(j * 64, (j + 1) * 64)
                        S.op("pe", [mm(PS(po)[:, pcs], V64[:, c, :], at[:, 0:64], True, False),
                                    mm(PS(po)[:, pcs], Sb[sbi], qh_[:, cj], False, True)],
                             reads=["V", "zpad", f"At{ai}", f"Sb{sbi}", okeys[0]], writes=[pk(po)])
                        pkv = 4 + ci % 2
                        S.op("pe", [mm(PS(pkv)[:, 0:128], kt_[:, :], V64[:, c, :], True, True)],
                             reads=[f"ktok{ai}", "V", "zpad"], writes=[pk(pkv)])
                        S.op("dve", [stt(Sst, Sst, glx_[:, j:j + 1], PS(pkv)[:, 0:128], ALU.mult, ALU.add)],
                             reads=[pk(pkv), f"glx{gs_}", "Sst"], writes=["Sst"])
                        sbi = 1 - sbi
                        S.op("act", [act(Sb[sbi], Sst, AF.Copy)], reads=["Sst"], writes=[f"Sb{sbi}"])
                    if HL < 4:
                        continue
                    cmin, cmax = min(chunks), max(chunks)
                    pc = slice((cmin - tb * 8) * 64, (cmax - tb * 8 + 1) * 64)
                    tc = slice(cmin * 64, (cmax + 1) * 64)
                    if fwd:
                        S.op("act", [act(oacc[:, tc], PS(po)[:, pc], AF.Copy)], reads=[pk(po)], writes=["oacc"])
                    else:
                        S.op("dve", [tt(oacc[:, tc], PS(po)[:, pc], oacc[:, tc], ALU.add)], reads=[pk(po), "oacc"], writes=["oacc"])
            for tb, (c0, n) in enumerate(TBS):
                cols = slice(c0, c0 + n)
                i = tb
                S.op("act", [act(sqo[i % 2][:, 0:n], oacc[:, cols], AF.Square)], reads=["oacc"], writes=[f"sqo{i % 2}"])
                S.op("pe", [mm(PS(i % 2)[:, 0:n], ones, sqo[i % 2][:, 0:n], True, True)], reads=[f"sqo{i % 2}", "c16"], writes=[pk(i % 2)])
                r = rr[i % 2]
                S.op("act", [act(r[:, 0:n], PS(i % 2)[:, 0:n], AF.Sqrt, bias=cst[:, 0:1], scale=1.0 / 128)],
                     reads=[pk(i % 2), "cst"], writes=[f"rr{i % 2}"])
                S.op("dve", [lambda e, r=r, n=n: e.reciprocal(out=r[:, 0:n], in_=r[:, 0:n])], reads=[f"rr{i % 2}"], writes=[f"rr{i % 2}"])
                S.op("pe", [mm(PS(2 + i % 2)[:, 0:n], wh[:, kc, 4, :], hT[:, kc, cols], kc == 0, kc == 7) for kc in range(8)],
                     reads=["wh", ("h", tb)], writes=[pk(2 + i % 2)])
                gsl_ = sg[i % 2]
                S.op("act", [act(gsl_[:, 0:n], PS(2 + i % 2)[:, 0:n], AF.Silu)], reads=[pk(2 + i % 2)], writes=[f"sg{i % 2}"])
                tq_ = Eb[i % 2]
                S.op("dve", [tt(tq_[:, 0:n], oacc[:, cols], r[:, 0:n], ALU.mult)], reads=["oacc", f"rr{i % 2}"], writes=[f"Eb{i % 2}"])
                S.op("dve", [stt(oT[:, hh, cols], tq_[:, 0:n], vecs[:, V_HGN + l:V_HGN + l + 1], gsl_[:, 0:n],
                                 ALU.mult, ALU.mult)], reads=[f"Eb{i % 2}", f"sg{i % 2}", "vecs"], writes=["oT"])
        S.barrier()
        A.top = mark

    for l in range(nlayers):
        norm_stage(l, 1)
        if dbg and l == 0:
            dump("h0", hT, [8, NT], reads=[("h", tb) for tb in range(5)])
        if stop_after == "norm1":
            break
        mark_l = A.top
        oT = A.new(BF16, 4, NT)
        stage_list = [("four", fourier_stage), ("attn", attention_stage), ("hgrn", hgrn_stage)]
        for n_br, (nm, fn) in enumerate(stage_list):
            if only is not None and nm not in only:
                continue
            fn(l, oT)
            if dbg and l == 0:
                dump("o_" + nm, oT, [4, NT], reads=["oT"])
                S.barrier()
            merge_stage(l, n_br, oT)
        A.top = mark_l
        if dbg and l == 0:
            dump("xmid", xT, [8, NT], reads=[])
            S.barrier()
        norm_stage(l, 2)
        ffn_stage(l)
        if dbg and l == 0:
            dump("xout", xT, [8, NT], reads=[])
            S.barrier()

    mark = A.top
    identf = A.new(F32, 128)
    ob = [A.new(F32, D), A.new(F32, D)]
    S.op("sp", [dma(identf, identf_d[:, :])], writes=["identf"], dsem="c_identf")
    for t in range(2, NTILE):
        b = t % 2
        for half in range(2):
            bi = (2 * t + half) % 4
            fns = [mm(PS(bi)[:, j * 128:(j + 1) * 128], xT[:, half * 4 + j, t * 128:(t + 1) * 128],
                      identf, True, True) for j in range(4)]
            S.op("pe", fns, reads=[("x", tile_tb(t)), "identf"], writes=[pk(bi)])
            if half == 0:
                S.op("act", [act(ob[b][:, 0:512], PS(bi), AF.Copy)], reads=[pk(bi)], writes=[f"ob{b}h0"])
            else:
                S.op("dve", [cp(ob[b][:, 512:1024], PS(bi))], reads=[pk(bi)], writes=[f"ob{b}h1"])
        S.op("sp", [dma(out_d[(t - 2) * 128:(t - 1) * 128, :], ob[b])], reads=[f"ob{b}h0", f"ob{b}h1"], dsem=f"st{b}")
    S.wait_all("sp", [k for k in S.cnt if k.startswith("st") or k == "dbg"])
    A.top = mark

    with nc.Block() as block:
        @block.tensor
        def _(e):
            S.replay("pe", e)

        @block.scalar
        def _(e):
            S.replay("act", e)

        @block.vector
        def _(e):
            S.replay("dve", e)

        @block.gpsimd
        def _(e):
            S.replay("pool", e)

        @block.sync
        def _(e):
            S.replay("sp", e)
    print('arena peak', A.peak, 'ops', S.nops)
    es.close()
    return nc, dbg_out


_CONST_CACHE = {}


def _consts():
    if "c" not in _CONST_CACHE:
        _CONST_CACHE["c"] = make_consts()
    return _CONST_CACHE["c"]


def run(inputs, nlayers=NL, dbg=False, stop_after=None, cores=8, only=None):
    c16, r16, dftc, dfts, identf = _consts()
    nc, dbg_out = build(nlayers=nlayers, dbg=dbg, stop_after=stop_after, only=only)
    f = lambda a: np.ascontiguousarray(np.asarray(a, dtype=np.float32))
    shared = {
        "identf": identf, "c16": c16, "r16": r16, "dftc": dftc, "dfts": dfts,
        "w_mod": f(inputs["w_mod"]), "w_in": f(inputs["w_in"]), "w_branch": f(inputs["w_branch"]),
        "w_out": f(inputs["w_out"]), "w_ff1": f(inputs["w_ff1"]), "w_ff2": f(inputs["w_ff2"]),
    }
    x = f(inputs["x"])
    ctx = f(inputs["ctx"])
    c = f(inputs["c"])
    in_maps = []
    for b in range(cores):
        m = dict(shared)
        m["x"] = x[b]
        m["ctx"] = ctx[b]
        m["vecs"] = make_vecs(c[b], f(inputs["c_ctx"]), f(inputs["b_mod"]), f(inputs["norm1_g"]),
                              f(inputs["norm2_g"]), f(inputs["q_norm_g"]), f(inputs["k_norm_g"]),
                              f(inputs["attn_sink"]), f(inputs["hgrn_lb_logits"]), f(inputs["hgrn_norm_g"]))
        in_maps.append(m)
    res = run_bass_kernel_spmd(nc, in_maps, core_ids=list(range(cores)))
    return res


ENABLED_MIXERS = ("four", "attn", "hgrn")


def kernel(**inputs):
    res = run(inputs, only=ENABLED_MIXERS)
    return np.stack([np.asarray(r["out"], dtype=np.float32) for r in res.results], axis=0)
```
